# Optimizing a Trainium2 kernel written in Bass

```python
import math
import jax, jax.numpy as jnp
from jax import lax
import numpy as np

D_MODEL = 4096
BATCH = 2
SEQ = 4096
DEPTH = 2

N_MIXERS = 2
N_ATTN_LAYERS = (DEPTH + 1) // 2
N_SSM_LAYERS = DEPTH // 2

HEAD_DIM = 64
N_Q_HEADS = D_MODEL // HEAD_DIM
N_KV_HEADS = 8
Q_PER_KV = N_Q_HEADS // N_KV_HEADS
ATTN_WIDTH = N_Q_HEADS * HEAD_DIM
KV_WIDTH = N_KV_HEADS * HEAD_DIM
ATTN_IN_WIDTH = 2 * ATTN_WIDTH + 2 * KV_WIDTH
WINDOW = 128
BLOCK = 128

SSM_EXPAND = 2
SSM_WIDTH = SSM_EXPAND * D_MODEL
GROUP_SIZE = 16
N_GROUPS = SSM_WIDTH // GROUP_SIZE
STATE_DIM = 64
SCAN_CHUNK = 128
DT_MIN = 0.001
DT_MAX = 0.1

RMS_EPS = 1e-6
NEG_INF = -1e30

kernel_name = "hybrid_swa_sink_alibi_s5_interleaved"


def rms_norm(x, g):
    xf = x.astype(jnp.float32)
    var = jnp.mean(xf * xf, axis=-1, keepdims=True)
    return (xf * lax.rsqrt(var + RMS_EPS) * g.astype(jnp.float32)).astype(x.dtype)


def alibi_slopes(n_heads):
    return jnp.exp2(-8.0 * jnp.arange(1, n_heads + 1, dtype=jnp.float32) / n_heads)


def attn_mixer(h, w_in, q_g, k_g, sinks, w_out):
    bsz, seq, _ = h.shape
    proj = h @ w_in
    q, k, v, gate = jnp.split(
        proj, [ATTN_WIDTH, ATTN_WIDTH + KV_WIDTH, ATTN_WIDTH + 2 * KV_WIDTH], axis=-1)
    q = rms_norm(q.reshape(bsz, seq, N_Q_HEADS, HEAD_DIM), q_g).astype(jnp.float32) * (HEAD_DIM ** -0.5)
    k = rms_norm(k.reshape(bsz, seq, N_KV_HEADS, HEAD_DIM), k_g).astype(jnp.float32)
    v = v.reshape(bsz, seq, N_KV_HEADS, HEAD_DIM).astype(jnp.float32)

    nb = seq // BLOCK
    qb = q.reshape(bsz, nb, BLOCK, N_KV_HEADS, Q_PER_KV, HEAD_DIM).transpose(1, 0, 3, 4, 2, 5)

    def band(t):
        tp = jnp.pad(t, ((0, 0), (BLOCK, 0), (0, 0), (0, 0)))
        prev = tp[:, :seq].reshape(bsz, nb, BLOCK, N_KV_HEADS, HEAD_DIM)
        cur = tp[:, BLOCK:].reshape(bsz, nb, BLOCK, N_KV_HEADS, HEAD_DIM)
        return jnp.concatenate([prev, cur], axis=2).transpose(1, 0, 3, 2, 4)

    kb, vb = band(k), band(v)
    slopes = alibi_slopes(N_Q_HEADS).reshape(N_KV_HEADS, Q_PER_KV)
    qi = jnp.arange(BLOCK)[:, None]
    kj = jnp.arange(2 * BLOCK)[None, :]
    dist = BLOCK + qi - kj
    in_window = (dist >= 0) & (dist < WINDOW)
    alibi = -slopes[:, :, None, None] * dist.astype(jnp.float32)
    sink = sinks.astype(jnp.float32).reshape(N_KV_HEADS, Q_PER_KV)[:, :, None, None]

    def one_block(args):
        n, qn, kn, vn = args
        valid = in_window & ((n - 1) * BLOCK + kj >= 0)
        s = jnp.einsum('bkgqd,bksd->bkgqs', qn, kn) + alibi
        s = jnp.where(valid, s, NEG_INF)
        m = jnp.maximum(jnp.max(s, axis=-1, keepdims=True), sink)
        p = jnp.exp(s - m)
        denom = jnp.sum(p, axis=-1, keepdims=True) + jnp.exp(sink - m)
        return jnp.einsum('bkgqs,bksd->bkgqd', p, vn) / denom

    o = lax.map(one_block, (jnp.arange(nb), qb, kb, vb))
    o = o.transpose(1, 0, 4, 2, 3, 5).reshape(bsz, seq, ATTN_WIDTH).astype(h.dtype)
    return (o * jax.nn.silu(gate)) @ w_out


def _scan_combine(e1, e2):
    a1, b1 = e1
    a2, b2 = e2
    return (a1 * a2, a2 * b1 + b2)


def ssm_mixer(h, w_in, log_dt, lam_re, lam_im, b_re, b_im, c_re, c_im, d_skip, w_glu, w_out):
    bsz, seq, _ = h.shape
    u, gate = jnp.split(h @ w_in, 2, axis=-1)
    uf = u.astype(jnp.float32)
    lam = lax.complex(lam_re.astype(jnp.float32), lam_im.astype(jnp.float32))
    dt = jnp.exp(log_dt.astype(jnp.float32))[:, None]
    a_bar = jnp.exp(lam * dt)
    b = lax.complex(b_re.astype(jnp.float32), b_im.astype(jnp.float32))
    b_bar = ((a_bar - 1.0) / lam)[..., None] * b
    b_bar_re, b_bar_im = jnp.real(b_bar), jnp.imag(b_bar)
    c = lax.complex(c_re.astype(jnp.float32), c_im.astype(jnp.float32))

    nc = seq // SCAN_CHUNK
    u_chunks = uf.reshape(bsz, nc, SCAN_CHUNK, N_GROUPS, GROUP_SIZE).transpose(1, 0, 2, 3, 4)

    def chunk_step(carry, u_c):
        bu = lax.complex(jnp.einsum('btgc,gpc->btgp', u_c, b_bar_re),
                         jnp.einsum('btgc,gpc->btgp', u_c, b_bar_im))
        a = jnp.broadcast_to(a_bar, bu.shape)
        a_cum, h_loc = lax.associative_scan(_scan_combine, (a, bu), axis=1)
        states = h_loc + a_cum * carry[:, None]
        y = jnp.real(jnp.einsum('btgp,gcp->btgc', states, c))
        return states[:, -1], y

    carry0 = jnp.zeros((bsz, N_GROUPS, STATE_DIM), dtype=jnp.complex64)
    _, ys = lax.scan(chunk_step, carry0, u_chunks)
    y = ys.transpose(1, 0, 2, 3, 4).reshape(bsz, seq, SSM_WIDTH) + d_skip.astype(jnp.float32) * uf
    y = jax.nn.gelu(y)
    y = y * jax.nn.sigmoid(y @ w_glu.astype(jnp.float32))
    y = y.astype(h.dtype) * jax.nn.silu(gate)
    return y @ w_out


def setup_inputs(seed: int = 0) -> dict:
    key = jax.random.key(seed)
    ks = jax.random.split(key, 20)
    f32 = jnp.float32
    na, ns = N_ATTN_LAYERS, N_SSM_LAYERS
    nrm = lambda k, shape, s: jax.random.normal(k, shape, f32) * s
    x = jax.random.normal(ks[0], (BATCH, SEQ, D_MODEL), f32)
    norm_g = 1.0 + nrm(ks[1], (DEPTH, D_MODEL), 0.01)
    attn_w_in = nrm(ks[2], (na, D_MODEL, ATTN_IN_WIDTH), D_MODEL ** -0.5)
    attn_q_norm_g = 1.0 + nrm(ks[3], (na, HEAD_DIM), 0.01)
    attn_k_norm_g = 1.0 + nrm(ks[4], (na, HEAD_DIM), 0.01)
    attn_sinks = nrm(ks[5], (na, N_Q_HEADS), 0.5)
    attn_w_out = nrm(ks[6], (na, ATTN_WIDTH, D_MODEL), ATTN_WIDTH ** -0.5)
    ssm_w_in = nrm(ks[7], (ns, D_MODEL, 2 * SSM_WIDTH), D_MODEL ** -0.5)
    ssm_log_dt = jax.random.uniform(ks[8], (ns, N_GROUPS), f32,
                                    math.log(DT_MIN), math.log(DT_MAX))
    ssm_lam_re = -0.5 + nrm(ks[9], (ns, N_GROUPS, STATE_DIM), 0.01)
    ssm_lam_im = (math.pi * jnp.arange(STATE_DIM, dtype=f32))[None, None, :] + nrm(
        ks[10], (ns, N_GROUPS, STATE_DIM), 0.01)
    ssm_b_re = nrm(ks[11], (ns, N_GROUPS, STATE_DIM, GROUP_SIZE), (2 * GROUP_SIZE) ** -0.5)
    ssm_b_im = nrm(ks[12], (ns, N_GROUPS, STATE_DIM, GROUP_SIZE), (2 * GROUP_SIZE) ** -0.5)
    ssm_c_re = nrm(ks[13], (ns, N_GROUPS, GROUP_SIZE, STATE_DIM), STATE_DIM ** -0.5)
    ssm_c_im = nrm(ks[14], (ns, N_GROUPS, GROUP_SIZE, STATE_DIM), STATE_DIM ** -0.5)
    ssm_d = nrm(ks[15], (ns, SSM_WIDTH), 1.0)
    ssm_w_glu = nrm(ks[16], (ns, SSM_WIDTH, SSM_WIDTH), SSM_WIDTH ** -0.5)
    ssm_w_out = nrm(ks[17], (ns, SSM_WIDTH, D_MODEL), SSM_WIDTH ** -0.5)
    return {"x": x, "norm_g": norm_g,
            "attn_w_in": attn_w_in, "attn_q_norm_g": attn_q_norm_g, "attn_k_norm_g": attn_k_norm_g,
            "attn_sinks": attn_sinks, "attn_w_out": attn_w_out,
            "ssm_w_in": ssm_w_in, "ssm_log_dt": ssm_log_dt, "ssm_lam_re": ssm_lam_re,
            "ssm_lam_im": ssm_lam_im, "ssm_b_re": ssm_b_re, "ssm_b_im": ssm_b_im,
            "ssm_c_re": ssm_c_re, "ssm_c_im": ssm_c_im, "ssm_d": ssm_d,
            "ssm_w_glu": ssm_w_glu, "ssm_w_out": ssm_w_out}


def reference(x, norm_g, attn_w_in, attn_q_norm_g, attn_k_norm_g, attn_sinks, attn_w_out,
              ssm_w_in, ssm_log_dt, ssm_lam_re, ssm_lam_im, ssm_b_re, ssm_b_im,
              ssm_c_re, ssm_c_im, ssm_d, ssm_w_glu, ssm_w_out):
    h = x
    for i in range(DEPTH):
        hn = rms_norm(h, norm_g[i])
        j = i // N_MIXERS
        if i % N_MIXERS == 0:
            out = attn_mixer(hn, attn_w_in[j], attn_q_norm_g[j], attn_k_norm_g[j],
                             attn_sinks[j], attn_w_out[j])
        else:
            out = ssm_mixer(hn, ssm_w_in[j], ssm_log_dt[j], ssm_lam_re[j], ssm_lam_im[j],
                            ssm_b_re[j], ssm_b_im[j], ssm_c_re[j], ssm_c_im[j], ssm_d[j],
                            ssm_w_glu[j], ssm_w_out[j])
        h = h + out.astype(h.dtype)
    return h
```

```python
import numpy as np
from contextlib import ExitStack
import concourse.bass as bass
import concourse.mybir as mybir
from concourse.bass_utils import run_bass_kernel_spmd

F32 = mybir.dt.float32
BF16 = mybir.dt.bfloat16
ALU = mybir.AluOpType
AF = mybir.ActivationFunctionType
AX = mybir.AxisListType
NCORES = 8
D = 4096
T = 1024
HALO = 128
TT = T + HALO
E = 8192
NPAIR = 256
EPS = 1e-6
MAGIC = 12582912.0
TWO_PI = float(2 * np.pi)
SLOPES = [float(2.0 ** (-8.0 * (h + 1) / 64.0)) for h in range(64)]


class Bld:
    def __init__(self):
        self.nc = bass.Bass("TRN2", target_bir_lowering=False)
        self.es = ExitStack()
        self.stack = [self.es]
        nc = self.nc
        self.eng = {"pe": nc.tensor, "act": nc.scalar, "dve": nc.vector, "pool": nc.gpsimd, "sp": nc.sync}
        self.sems, self.cnt, self.waited = {}, {}, {}
        for k in self.eng:
            self.newsem(k)
        self.dram = {}

    def push(self):
        self.stack.append(ExitStack())

    def pop(self):
        self.stack.pop().close()

    def newsem(self, key):
        if key in self.sems:
            return key
        self.sems[key] = self.es.enter_context(self.nc.semaphore(key))
        self.cnt[key] = 0
        return key

    def sb(self, name, shape, dt):
        return self.stack[-1].enter_context(self.nc.sbuf_tensor(name, shape, dt))

    def ps(self, name, shape, dt):
        return self.stack[-1].enter_context(self.nc.psum_tensor(name, shape, dt))

    def din(self, name, shape, dt=F32):
        self.dram[name] = self.nc.dram_tensor(name, list(shape), dt, kind="ExternalInput").ap()
        return self.dram[name]

    def dout(self, name, shape, dt=F32):
        self.dram[name] = self.nc.dram_tensor(name, list(shape), dt, kind="ExternalOutput").ap()
        return self.dram[name]

    def dscr(self, name, shape, dt, external=False):
        kind = "ExternalOutput" if external else "Internal"
        self.dram[name] = self.nc.dram_tensor(name, list(shape), dt, kind=kind).ap()
        return self.dram[name]

    def I(self, e, ins):
        self.cnt[e] += 1
        ins.then_inc(self.sems[e], 1)
        return (e, self.cnt[e])

    def last(self, e):
        return (e, self.cnt[e])

    def W(self, e, *tickets):
        for t in tickets:
            if t is None:
                continue
            if isinstance(t, list):
                self.W(e, *t)
                continue
            k, c = t
            if k == e or c <= 0 or self.waited.get((e, k), 0) >= c:
                continue
            self.waited[(e, k)] = c
            self.eng[e].wait_ge(self.sems[k], c)

    def Wf(self, e, t):
        k, c = t
        self.waited[(e, k)] = max(c, self.waited.get((e, k), 0))
        self.eng[e].wait_ge(self.sems[k], c)

    def dma(self, q, semkey, out, in_):
        self.newsem(semkey)
        ins = self.eng[q].dma_start(out=out, in_=in_)
        self.cnt[semkey] += 16
        ins.then_inc(self.sems[semkey], 16)
        return (semkey, self.cnt[semkey])

    def barrier(self):
        tick = [(k, c) for k, c in self.cnt.items() if c > 0]
        for e in self.eng:
            self.W(e, *tick)


def stage_consts(b):
    nc = b.nc
    c = {}
    c["identb"] = b.sb("identb", [128, 128], BF16)
    c["identf"] = b.sb("identf", [128, 128], F32)
    c["onesb"] = b.sb("onesb", [128, 128], BF16)
    c["blk64"] = b.sb("blk64", [128, 128], BF16)
    tmp = b.sb("iota_tmp", [128, 128], F32)
    b.I("pool", nc.gpsimd.iota(tmp[:], [[1, 128]], base=0, channel_multiplier=-1, allow_small_or_imprecise_dtypes=True))
    b.I("pool", nc.gpsimd.tensor_scalar(c["identf"][:], tmp[:], 0.0, None, ALU.is_equal))
    b.I("pool", nc.gpsimd.tensor_copy(c["identb"][:], c["identf"][:]))
    b.I("pool", nc.gpsimd.memset(c["onesb"][:], 1.0))
    b.I("pool", nc.gpsimd.memset(c["blk64"][:], 0.0))
    b.I("pool", nc.gpsimd.memset(c["blk64"][0:64, 0:64], 1.0))
    b.I("pool", nc.gpsimd.memset(c["blk64"][64:128, 64:128], 1.0))
    c["ready"] = b.last("pool")
    return c


def norm_T(b, c, src, ntiles, gcol, hT, tag):
    nc = b.nc
    b.push()
    xt = [b.sb(f"{tag}_xt{i}", [128, D], F32) for i in range(2)]
    junk = b.sb(f"{tag}_junk", [128, D], BF16)
    xs = [b.sb(f"{tag}_xs{i}", [128, D], BF16) for i in range(2)]
    ss = b.sb(f"{tag}_ss", [128, 8], F32)
    pst = [b.ps(f"{tag}_pst{i}", [128, 8, 128], BF16) for i in range(2)]
    xs_free = [None, None]
    xt_free = [None, None]
    pst_free = [None, None]
    b.W("pe", c["ready"])
    for i in range(ntiles):
        s = i % 2
        b.W("sp", xt_free[s])
        tl = b.dma("sp", f"nt_ld{s}", xt[s][:], src[i * 128:(i + 1) * 128, :])
        b.W("act", tl)
        b.I("act", nc.scalar.activation(junk[:], xt[s][:], AF.Square, accum_out=ss[:, s:s + 1]))
        t1 = b.last("act")
        b.W("dve", t1)
        ta = b.I("dve", nc.vector.tensor_scalar(ss[:, 2 + s:3 + s], ss[:, s:s + 1], 1.0 / D, EPS, ALU.mult, ALU.add))
        b.W("act", ta)
        tb = b.I("act", nc.scalar.activation(ss[:, 6 + s:7 + s], ss[:, 2 + s:3 + s], AF.Sqrt))
        b.W("dve", tb)
        trc = b.I("dve", nc.vector.reciprocal(ss[:, 4 + s:5 + s], ss[:, 6 + s:7 + s]))
        b.Wf("dve", trc)
        b.W("dve", xs_free[s])
        t2 = b.I("dve", nc.vector.tensor_scalar(xs[s][:], xt[s][:], ss[:, 4 + s:5 + s], None, ALU.mult))
        xt_free[s] = t2
        for g4 in range(4):
            pp = (i * 4 + g4) % 2
            b.W("pe", t2, pst_free[pp])
            for j in range(8):
                cc = g4 * 8 + j
                b.I("pe", nc.tensor.transpose(pst[pp][:, j, :], xs[s][:, cc * 128:(cc + 1) * 128], c["identb"][:]))
            t3 = b.last("pe")
            gb = gcol[:, g4 * 8:(g4 + 1) * 8].unsqueeze(2).to_broadcast([128, 8, 128])
            b.W("dve", t3)
            t4 = b.I("dve", nc.vector.tensor_tensor(hT[:, g4 * 8:(g4 + 1) * 8, i * 128:(i + 1) * 128], pst[pp][:], gb, ALU.mult))
            pst_free[pp] = t4
        xs_free[s] = b.last("pe")
    b.barrier()
    b.pop()


def fm_matmul(b, w_dram, KC, chunks, hT, pieces, epilogue, tag, nslot=3):
    nc = b.nc
    wb = [b.sb(f"{tag}_wb{i}", [128, KC, 128], BF16) for i in range(nslot)]
    pss = [b.ps(f"{tag}_ps{i}", [128, 3, 512], F32) for i in range(2)]
    wb_free = [None] * nslot
    ps_free = [None, None]
    wv = w_dram.rearrange("(kc p) n -> p kc n", p=128)
    loads = {}
    order = list(chunks)

    def issue(ix):
        n = order[ix]
        s = ix % nslot
        b.W("pool", wb_free[s])
        loads[ix] = b.dma("pool", f"{tag}_wld{s}", wb[s][:], wv[:, :, n * 128:(n + 1) * 128])

    for ix in range(min(nslot - 1, len(order))):
        issue(ix)
    for ix, n in enumerate(order):
        if ix + nslot - 1 < len(order):
            issue(ix + nslot - 1)
        s = ix % nslot
        p = ix % 2
        b.W("pe", loads[ix], ps_free[p])
        for j, (t0, nt) in enumerate(pieces(n)):
            for kc in range(KC):
                b.I("pe", nc.tensor.matmul(pss[p][:, j, 0:nt], wb[s][:, kc, :], hT[:, kc, t0:t0 + nt],
                                           start=(kc == 0), stop=(kc == KC - 1)))
        tpe = b.last("pe")
        wb_free[s] = tpe
        ps_free[p] = epilogue(n, ix, pss[p], tpe)


def tm_matmul(b, w_dram, KC, nslab, slabw, aT, ntt, epilogue, tag):
    nc = b.nc
    wb = [b.sb(f"{tag}_wb{i}", [128, KC, slabw], BF16) for i in range(2)]
    pss = [b.ps(f"{tag}_ps{i}", [128, 512], F32) for i in range(2)]
    wb_free = [None, None]
    ps_free = [None, None]
    wv = w_dram.rearrange("(kc p) n -> p kc n", p=128)
    loads = {}

    def issue(s_):
        sl = s_ % 2
        b.W("pool", wb_free[sl])
        loads[s_] = b.dma("pool", f"{tag}_wld{sl}", wb[sl][:], wv[:, :, s_ * slabw:(s_ + 1) * slabw])

    issue(0)
    it = 0
    for s_ in range(nslab):
        if s_ + 1 < nslab:
            issue(s_ + 1)
        sl = s_ % 2
        for t in range(ntt):
            p = it % 2
            it += 1
            b.W("pe", loads[s_], ps_free[p])
            for kc in range(KC):
                b.I("pe", nc.tensor.matmul(pss[p][:, 0:slabw], aT[:, kc, t * 128:(t + 1) * 128], wb[sl][:, kc, :],
                                           start=(kc == 0), stop=(kc == KC - 1)))
            tpe = b.last("pe")
            ps_free[p] = epilogue(s_, t, pss[p], tpe)
        wb_free[sl] = b.last("pe")


def layer0(b, c, dbg=False):
    nc = b.nc
    xh = b.din("xh", [TT, D])
    ng0 = b.din("ng0c", [128, 32])
    w_in = b.din("attn_w_in", [D, 9216])
    w_out = b.din("attn_w_out", [D, D])
    gqk = b.din("gqk", [128, 2])
    sinkc = b.din("sinkc", [128, 32])
    cdist = b.din("cdist", [128, 3, 128])
    h1 = b.dram["h1"]
    QT = b.dscr("QT_d", [D, T], BF16, external=dbg)
    GT = b.dscr("GT_d", [D, T], BF16, external=dbg)
    KT = b.dscr("KT_d", [512, TT], BF16, external=dbg)

    b.push()
    big1 = b.sb("big1", [128, 32 * TT], BF16)
    hT = big1[:].rearrange("p (c t) -> p c t", t=TT)
    OGT = big1[:, 0:32 * T].rearrange("p (c t) -> p c t", t=T)
    gcol = b.sb("gcol0", [128, 32], F32)
    gq = b.sb("gqk_s", [128, 2], F32)
    esk = b.sb("esk", [128, 32], F32)
    dist = b.sb("dist", [128, 3, 128], F32)
    Vtok = b.sb("Vtok", [128, 9, 512], BF16)
    t = b.dma("sp", "misc", gcol[:], ng0)
    t = b.dma("sp", "misc", gq[:], gqk)
    t = b.dma("sp", "misc", esk[:], sinkc)
    t = b.dma("sp", "misc", dist[:], cdist)
    b.W("act", t)
    b.I("act", nc.scalar.activation(esk[:], esk[:], AF.Exp))
    b.W("dve", t)

    norm_T(b, c, xh, TT // 128, gcol, hT, "n0")

    b.push()
    sqb = [b.sb(f"sqb{i}", [128, 512], BF16) for i in range(2)]
    rr = [b.sb(f"rr{i}", [128, 512], F32) for i in range(2)]
    qst = [b.sb(f"qst{i}", [128, TT], BF16) for i in range(2)]
    vT = b.sb("vT", [128, TT], F32)
    ps2 = [b.ps(f"ps2_{i}", [128, 512], F32) for i in range(2)]
    st_free = [None, None]
    ps2_free = [None, None]
    state = {"k": 0, "q": 0}

    def pieces(n):
        if 32 <= n < 40:
            return [(0, 384), (384, 384), (768, 384)]
        return [(HALO, 512), (HALO + 512, 512)]

    def epi(n, ix, ps, tpe):
        pcs = pieces(n)
        if n < 36:
            isq = n < 32
            qs = state["q"] % 2
            state["q"] += 1
            b.W("dve", st_free[qs])
            for j, (t0, nt) in enumerate(pcs):
                k = state["k"] % 2
                state["k"] += 1
                b.W("act", tpe)
                t1 = b.I("act", nc.scalar.activation(sqb[k][:, 0:nt], ps[:, j, 0:nt], AF.Square))
                b.W("pe", t1, ps2_free[k])
                t2 = b.I("pe", nc.tensor.matmul(ps2[k][:, 0:nt], c["blk64"][:], sqb[k][:, 0:nt], start=True, stop=True))
                b.W("dve", t2)
                if isq:
                    ps2_free[k] = b.I("dve", nc.vector.tensor_scalar(rr[k][:, 0:nt], ps2[k][:, 0:nt], 1.0, 64.0 * EPS, ALU.mult, ALU.add))
                else:
                    ps2_free[k] = b.I("dve", nc.vector.tensor_scalar(rr[k][:, 0:nt], ps2[k][:, 0:nt], 1.0 / 64.0, EPS, ALU.mult, ALU.add))
                b.W("act", ps2_free[k])
                tsq = b.I("act", nc.scalar.activation(rr[k][:, 0:nt], rr[k][:, 0:nt], AF.Sqrt))
                b.W("dve", tsq)
                b.I("dve", nc.vector.reciprocal(rr[k][:, 0:nt], rr[k][:, 0:nt]))
                gsc = gq[:, 0:1] if isq else gq[:, 1:2]
                o0 = t0 - (HALO if isq else 0)
                t3 = b.I("dve", nc.vector.scalar_tensor_tensor(qst[qs][:, o0:o0 + nt], ps[:, j, 0:nt], gsc, rr[k][:, 0:nt], ALU.mult, ALU.mult))
            b.W("sp", t3)
            if isq:
                st_free[qs] = b.dma("sp", f"qst{qs}", QT[n * 128:(n + 1) * 128, :], qst[qs][:, 0:T])
            else:
                st_free[qs] = b.dma("sp", f"qst{qs}", KT[(n - 32) * 128:(n - 31) * 128, :], qst[qs][:, :])
            return [t3]
        if n < 40:
            b.W("act", tpe)
            for j, (t0, nt) in enumerate(pcs):
                t1 = b.I("act", nc.scalar.copy(vT[:, t0:t0 + nt], ps[:, j, 0:nt]))
            for r in range(3):
                k = state["k"] % 2
                state["k"] += 1
                b.W("pe", t1, ps2_free[k])
                for i3 in range(3):
                    tt_ = r * 3 + i3
                    t2 = b.I("pe", nc.tensor.transpose(ps2[k][:, i3 * 128:(i3 + 1) * 128], vT[:, tt_ * 128:(tt_ + 1) * 128], c["identf"][:]))
                b.W("dve", t2)
                t3 = b.I("dve", nc.vector.tensor_copy(Vtok[:, r * 3:r * 3 + 3, (n - 36) * 128:(n - 35) * 128],
                                                      ps2[k][:, 0:384].rearrange("p (a f) -> p a f", f=128)))
                ps2_free[k] = t3
            return [t1]
        qs = state["q"] % 2
        state["q"] += 1
        b.W("act", tpe, st_free[qs])
        for j, (t0, nt) in enumerate(pcs):
            t1 = b.I("act", nc.scalar.activation(qst[qs][:, t0 - HALO:t0 - HALO + nt], ps[:, j, 0:nt], AF.Silu))
        b.W("sp", t1)
        st_free[qs] = b.dma("sp", f"qst{qs}", GT[(n - 40) * 128:(n - 39) * 128, :], qst[qs][:, 0:T])
        return [t1]

    fm_matmul(b, w_in, 32, list(range(32, 40)) + list(range(0, 32)) + list(range(40, 72)), hT, pieces, epi, "inp")
    b.barrier()
    b.pop()

    b.push()
    KT2 = b.sb("KT2", [128, 8, TT], BF16)
    kv = KT.rearrange("(k d) t -> d k t", d=64)
    b.dma("sp", "misc", KT2[0:64, :, :], kv)
    tk = b.dma("sp", "misc", KT2[64:128, :, :], kv)
    Qg = [b.sb(f"Qg{i}", [128, 4, T], BF16) for i in range(2)]
    Gg = [b.sb(f"Gg{i}", [128, 4, T], BF16) for i in range(2)]
    bia = [b.sb(f"bia{i}", [128, 2, 4, 128], F32) for i in range(2)]
    Pm = [b.sb(f"Pm{i}", [128, 2, 512], BF16) for i in range(2)]
    den = b.sb("den", [128, 4, 128], F32)
    o1 = b.sb("o1", [128, 4, 128], F32)
    ps_s = [b.ps(f"ps_s{i}", [128, 2, 512], F32) for i in range(2)]
    ps_o = b.ps("ps_o", [128, 512], F32)
    ps_r = b.ps("ps_r", [128, 512], F32)
    qg_free = [None, None]
    pss_free = [None, None]
    bia_free = [None, None]
    pm_free = [None, None]
    po_free = None
    o1_free = None
    b.W("pe", tk)
    for k in range(8):
        s = k % 2
        b.W("sp", qg_free[s])
        b.dma("sp", f"qg{s}", Qg[s][:], QT[512 * k:512 * (k + 1), :].rearrange("(c p) t -> p c t", p=128))
        tq = b.dma("sp", f"qg{s}", Gg[s][:], GT[512 * k:512 * (k + 1), :].rearrange("(c p) t -> p c t", p=128))
        for bq in range(8):
            for half in range(2):
                hp = slice(64 * half, 64 * half + 64)
                b.W("pe", tq, pss_free[half])
                for wi, kt in enumerate((bq, bq + 1)):
                    b.I("pe", nc.tensor.matmul(ps_s[half][:, wi, :], KT2[hp, k, kt * 128:(kt + 1) * 128],
                                               Qg[s][hp, :, bq * 128:(bq + 1) * 128], start=True, stop=True))
                ts = b.last("pe")
                b.W("dve", ts, bia_free[half])
                for wi in range(2):
                    dsel = 2 if (wi == 0 and bq == 0) else wi
                    for cc in range(4):
                        head = 8 * k + 2 * cc + half
                        b.I("dve", nc.vector.scalar_tensor_tensor(bia[half][:, wi, cc, :], dist[:, dsel, :], -SLOPES[head],
                                                                  ps_s[half][:, wi, cc * 128:(cc + 1) * 128], ALU.mult, ALU.add))
                tb = b.last("dve")
                pss_free[half] = tb
                b.W("act", tb, pm_free[half])
                te = b.I("act", nc.scalar.activation(Pm[half][:].rearrange("p a f -> p (a f)"),
                                                     bia[half][:].rearrange("p a c f -> p (a c f)"), AF.Exp))
                bia_free[half] = te
                b.W("pe", te)
                if half == 0:
                    b.W("pe", po_free)
                for wi, kt in enumerate((bq, bq + 1)):
                    b.I("pe", nc.tensor.matmul(ps_o[hp, :], Vtok[:, kt, 64 * k:64 * k + 64], Pm[half][:, wi, :],
                                               start=(wi == 0), stop=(wi == 1), tile_position=(0, 64 * half)))
                for wi in range(2):
                    b.I("pe", nc.tensor.matmul(ps_r[hp, :], c["onesb"][:, 0:64], Pm[half][:, wi, :],
                                               start=(wi == 0), stop=(wi == 1), tile_position=(0, 64 * half)))
                pm_free[half] = b.last("pe")
            tp = b.last("pe")
            b.W("dve", tp, o1_free)
            b.I("dve", nc.vector.tensor_tensor(den[:], ps_r[:].rearrange("p (c f) -> p c f", f=128),
                                               esk[:, 4 * k:4 * k + 4].unsqueeze(2).to_broadcast([128, 4, 128]), ALU.add))
            b.I("dve", nc.vector.reciprocal(den[:], den[:]))
            to = b.I("dve", nc.vector.tensor_tensor(o1[:], ps_o[:].rearrange("p (c f) -> p c f", f=128), den[:], ALU.mult))
            po_free = to
            b.W("pool", to)
            o1_free = b.I("pool", nc.gpsimd.tensor_tensor(OGT[:, 4 * k:4 * k + 4, bq * 128:(bq + 1) * 128], o1[:],
                                                          Gg[s][:, :, bq * 128:(bq + 1) * 128], ALU.mult))
        qg_free[s] = [b.last("pe"), b.last("pool")]
    b.barrier()
    b.pop()

    b.push()
    xr = [b.sb(f"xr{i}", [128, 512], F32) for i in range(2)]
    hr = [b.sb(f"hr{i}", [128, 512], F32) for i in range(2)]
    xr_free = [None, None]
    hr_free = [None, None]
    st = {"i": 0}

    def epi_out(s_, t, ps, tpe):
        i = st["i"] % 2
        st["i"] += 1
        b.W("sp", xr_free[i])
        tl = b.dma("sp", f"xr{i}", xr[i][:], xh[HALO + t * 128:HALO + (t + 1) * 128, s_ * 512:(s_ + 1) * 512])
        b.W("dve", tpe, tl, hr_free[i])
        ta = b.I("dve", nc.vector.tensor_tensor(hr[i][:], ps[:, 0:512], xr[i][:], ALU.add))
        xr_free[i] = ta
        b.W("sp", ta)
        hr_free[i] = b.dma("sp", f"hr{i}", h1[t * 128:(t + 1) * 128, s_ * 512:(s_ + 1) * 512], hr[i][:])
        return [ta]

    tm_matmul(b, w_out, 32, 8, 512, OGT, 8, epi_out, "outp")
    b.barrier()
    b.pop()
    b.pop()


def build_l0(dbg=False):
    b = Bld()
    b.dout("h1", [T, D])
    c = stage_consts(b)
    layer0(b, c, dbg)
    b.barrier()
    return b


def cdist_const(first):
    qi = np.arange(128)[None, :]
    kj = np.arange(128)[:, None]
    dprev = (128 + qi - kj).astype(np.float32)
    dcur = (qi - kj).astype(np.float32)
    big = np.float32(1e9)
    mp = np.where((dprev >= 0) & (dprev < 128), dprev, big)
    mc = np.where((dcur >= 0) & (dcur < 128), dcur, big)
    m0 = np.full((128, 128), big, np.float32) if first else mp
    return np.ascontiguousarray(np.stack([mp, mc, m0], axis=1), np.float32)


def l0_inputs(core, x, norm_g, attn_w_in, attn_q_norm_g, attn_k_norm_g, attn_sinks, attn_w_out):
    bi, ch = divmod(core, 4)
    xh = np.zeros((TT, D), np.float32)
    xh[HALO:] = x[bi, ch * T:(ch + 1) * T]
    if ch > 0:
        xh[:HALO] = x[bi, ch * T - HALO:ch * T]
    gqk = np.stack([np.tile(attn_q_norm_g[0], 2), np.tile(attn_k_norm_g[0], 2)], axis=1).astype(np.float32)
    sk = attn_sinks[0].reshape(32, 2)
    sinkc = np.repeat(sk.T, 64, axis=0).astype(np.float32)
    return {
        "xh": xh,
        "ng0c": np.ascontiguousarray(norm_g[0].reshape(32, 128).T, np.float32),
        "attn_w_in": attn_w_in[0], "attn_w_out": attn_w_out[0],
        "gqk": np.ascontiguousarray(gqk), "sinkc": np.ascontiguousarray(sinkc),
        "cdist": cdist_const(ch == 0),
    }


def ssm_setup(b, c, need_c):
    nc = b.nc
    ldt = b.din("ssm_ldt", [128, NPAIR])
    lre_d = b.din("ssm_lre", [128, NPAIR])
    lim_d = b.din("ssm_lim", [128, NPAIR])
    bt_re = b.din("ssm_bt_re", [128, 64, 128])
    bt_im = b.din("ssm_bt_im", [128, 64, 128])
    S = {}
    S["thn"] = b.sb("thn", [128, NPAIR], F32)
    S["r"] = b.sb("rmag", [128, NPAIR], F32)
    S["BTr"] = b.sb("BTr", [128, 64, 128], BF16)
    S["BTi"] = b.sb("BTi", [128, 64, 128], BF16)
    S["iota1"] = b.sb("iota1", [128, 512], F32)
    S["cst"] = b.sb("cst", [128, 4], F32)
    S["ar"] = b.sb("a_re", [128, NPAIR], F32)
    S["ai"] = b.sb("a_im", [128, NPAIR], F32)
    lre = b.sb("lre", [128, NPAIR], F32)
    lim = b.sb("lim", [128, NPAIR], F32)
    dt = b.sb("dt", [128, NPAIR], F32)
    tmp = [b.sb(f"sst{i}", [128, NPAIR], F32) for i in range(4)]
    S["lre"], S["lim"], S["tmp"] = lre, lim, tmp
    b.dma("pool", "ssm_bt", S["BTr"][:], bt_re)
    S["bt_ready"] = b.dma("pool", "ssm_bt", S["BTi"][:], bt_im)
    b.dma("sp", "misc", dt[:], ldt)
    b.dma("sp", "misc", lre[:], lre_d)
    t = b.dma("sp", "misc", lim[:], lim_d)
    b.I("pool", nc.gpsimd.iota(S["iota1"][:], [[1, 512]], base=1, channel_multiplier=0, allow_small_or_imprecise_dtypes=True))
    b.I("pool", nc.gpsimd.memset(S["cst"][:, 0:1], float(np.pi / 2)))
    tp = b.I("pool", nc.gpsimd.memset(S["cst"][:, 1:2], 0.0))
    b.W("act", t, tp)
    b.W("dve", t, tp)
    t1 = b.I("act", nc.scalar.activation(dt[:], dt[:], AF.Exp))
    b.W("dve", t1)
    b.I("dve", nc.vector.tensor_tensor(tmp[0][:], lre[:], dt[:], ALU.mult))
    t2 = b.I("dve", nc.vector.tensor_tensor(tmp[1][:], lim[:], dt[:], ALU.mult))
    b.W("act", t2)
    t3 = b.I("act", nc.scalar.activation(S["r"][:], tmp[0][:], AF.Exp))
    b.I("dve", nc.vector.tensor_scalar(S["thn"][:], tmp[1][:], 1.0 / TWO_PI, None, ALU.mult))
    b.I("dve", nc.vector.tensor_scalar(tmp[2][:], S["thn"][:], MAGIC, None, ALU.add))
    b.I("dve", nc.vector.tensor_scalar(tmp[2][:], tmp[2][:], -MAGIC, None, ALU.add))
    t4 = b.I("dve", nc.vector.tensor_tensor(tmp[2][:], S["thn"][:], tmp[2][:], ALU.subtract))
    b.W("act", t4)
    b.I("act", nc.scalar.activation(tmp[3][:], tmp[2][:], AF.Sin, scale=TWO_PI))
    b.I("act", nc.scalar.activation(tmp[2][:], tmp[2][:], AF.Abs))
    t5 = b.I("act", nc.scalar.activation(tmp[2][:], tmp[2][:], AF.Sin, scale=-TWO_PI, bias=S["cst"][:, 0:1]))
    b.W("dve", t5, t3)
    b.I("dve", nc.vector.tensor_tensor(S["ar"][:], tmp[2][:], S["r"][:], ALU.mult))
    S["ready"] = b.I("dve", nc.vector.tensor_tensor(S["ai"][:], tmp[3][:], S["r"][:], ALU.mult))
    return S


def ssm_csetup(b, c, S):
    nc = b.nc
    cs_re = b.din("ssm_cs_re", [128, NPAIR, 32])
    cs_im = b.din("ssm_cs_im", [128, NPAIR, 32])
    S["CTr"] = b.sb("CTr", [128, NPAIR, 32], BF16)
    S["CTi"] = b.sb("CTi", [128, NPAIR, 32], BF16)
    b.push()
    NQ4 = NPAIR // 4
    cr = b.sb("cs_r", [128, NQ4, 32], F32)
    ci = b.sb("cs_i", [128, NQ4, 32], F32)
    w1 = b.sb("cs_w1", [128, NQ4, 32], F32)
    w2 = b.sb("cs_w2", [128, NQ4, 32], F32)
    cf = [b.sb(f"cf{i}", [128, NPAIR], F32) for i in range(6)]
    lre, lim, ar, ai = S["lre"], S["lim"], S["ar"], S["ai"]
    b.W("dve", S["ready"])
    V = nc.vector
    b.I("dve", V.tensor_scalar(cf[0][:], ar[:], -1.0, None, ALU.add))
    b.I("dve", V.tensor_tensor(cf[1][:], lre[:], lre[:], ALU.mult))
    b.I("dve", V.tensor_tensor(cf[2][:], lim[:], lim[:], ALU.mult))
    b.I("dve", V.tensor_tensor(cf[1][:], cf[1][:], cf[2][:], ALU.add))
    b.I("dve", V.reciprocal(cf[1][:], cf[1][:]))
    b.I("dve", V.tensor_tensor(cf[2][:], cf[0][:], lre[:], ALU.mult))
    b.I("dve", V.tensor_tensor(cf[3][:], ai[:], lim[:], ALU.mult))
    b.I("dve", V.tensor_tensor(cf[2][:], cf[2][:], cf[3][:], ALU.add))
    b.I("dve", V.tensor_tensor(cf[4][:], cf[2][:], cf[1][:], ALU.mult))
    b.I("dve", V.tensor_tensor(cf[2][:], ai[:], lre[:], ALU.mult))
    b.I("dve", V.tensor_tensor(cf[3][:], cf[0][:], lim[:], ALU.mult))
    b.I("dve", V.tensor_tensor(cf[2][:], cf[2][:], cf[3][:], ALU.subtract))
    b.I("dve", V.tensor_tensor(cf[5][:], cf[2][:], cf[1][:], ALU.mult))
    for hq in range(4):
        qs = slice(hq * NQ4, (hq + 1) * NQ4)
        b.W("sp", b.last("dve"))
        b.dma("sp", "misc", cr[:], cs_re[:, qs, :])
        t = b.dma("sp", "misc", ci[:], cs_im[:, qs, :])
        b.W("dve", t)
        cre = cf[4][:, qs].unsqueeze(2).to_broadcast([128, NQ4, 32])
        cim = cf[5][:, qs].unsqueeze(2).to_broadcast([128, NQ4, 32])
        b.I("dve", V.tensor_tensor(w1[:], cr[:], cre, ALU.mult))
        b.I("dve", V.tensor_tensor(w2[:], ci[:], cim, ALU.mult))
        b.I("dve", V.tensor_tensor(S["CTr"][:, qs, :], w1[:], w2[:], ALU.subtract))
        b.I("dve", V.tensor_tensor(w1[:], cr[:], cim, ALU.mult))
        b.I("dve", V.tensor_tensor(w2[:], ci[:], cre, ALU.mult))
        b.I("dve", V.tensor_tensor(w1[:], w1[:], w2[:], ALU.add))
        S["c_ready"] = b.I("dve", V.tensor_scalar(S["CTi"][:, qs, :], w1[:], -1.0, None, ALU.mult))
    b.barrier()
    b.pop()


def ssm_scan(b, c, S, UT, phaseB, init, E_out=None, YT=None, dcol=None):
    nc = b.nc
    V = nc.vector
    b.push()
    ut = [b.sb(f"ut{i}", [128, T], BF16) for i in range(2)]
    Cn = [b.sb(f"Cn{i}", [128, 512], F32) for i in range(2)]
    Sn = [b.sb(f"Sn{i}", [128, 512], F32) for i in range(2)]
    fv = b.sb("fv", [128, 512], F32)
    fk = b.sb("fk", [128, 512], F32)
    w = [b.sb(f"wk{i}", [128, 512], F32) for i in range(4)]
    wre = [b.sb(f"wre{i}", [128, 512], F32) for i in range(2)]
    wim = [b.sb(f"wim{i}", [128, 512], F32) for i in range(2)]
    xe = b.sb("xe", [128, 8], F32)
    zero = b.sb("zero2", [128, 2], F32)
    ps_b = [b.ps(f"ps_b{i}", [128, 2, 512], F32) for i in range(2)]
    if phaseB:
        pw = [b.sb(f"pw{i}", [128, 512], F32) for i in range(4)]
        xr_ = [b.sb(f"xrb{i}", [128, 512], BF16) for i in range(2)]
        xi_ = [b.sb(f"xib{i}", [128, 512], BF16) for i in range(2)]
        ps_y = [b.ps(f"ps_y{i}", [128, 512], F32) for i in range(2)]
        yf = [b.sb(f"yf{i}", [128, 512], F32) for i in range(2)]
        yb = [b.sb(f"yb{i}", [128, 512], BF16) for i in range(2)]
    b.I("dve", V.memset(zero[:], 0.0))
    b.W("pe", S["bt_ready"])
    ut_free = [None, None]
    tab_free = [None, None]
    psb_free = [None, None]
    w_free = [None, None]
    x_free = [None, None]
    psy_free = [None, None]
    yb_free = [None, None]
    yf_free = [None, None]
    it = 0
    for tile in range(64):
        us = tile % 2
        b.W("sp", ut_free[us])
        tu = b.dma("sp", f"ut{us}", ut[us][:], UT[tile * 128:(tile + 1) * 128, :])
        if phaseB:
            b.W("pe", psy_free[0], psy_free[1])
        for q4 in range(4):
            q = tile * 4 + q4
            ts_ = q % 2
            b.W("dve", S["ready"], tab_free[ts_], b.last("act"))
            b.I("dve", V.tensor_scalar(fv[:], S["iota1"][:], S["thn"][:, q:q + 1], None, ALU.mult))
            b.I("dve", V.tensor_scalar(fk[:], fv[:], MAGIC, None, ALU.add))
            b.I("dve", V.tensor_scalar(fk[:], fk[:], -MAGIC, None, ALU.add))
            tf = b.I("dve", V.tensor_tensor(fv[:], fv[:], fk[:], ALU.subtract))
            b.W("act", tf)
            b.I("act", nc.scalar.activation(Sn[ts_][:], fv[:], AF.Sin, scale=TWO_PI))
            b.I("act", nc.scalar.activation(Cn[ts_][:], fv[:], AF.Abs))
            ttab = b.I("act", nc.scalar.activation(Cn[ts_][:], Cn[ts_][:], AF.Sin, scale=-TWO_PI, bias=S["cst"][:, 0:1]))
            for ps_ in range(2):
                pb = it % 2
                it += 1
                tok = slice(ps_ * 512, (ps_ + 1) * 512)
                rows = slice(32 * q4, 32 * q4 + 32)
                b.W("pe", tu, psb_free[pb])
                b.I("pe", nc.tensor.matmul(ps_b[pb][:, 0, :], S["BTr"][rows, tile, :], ut[us][rows, tok], start=True, stop=True,
                                           tile_position=(32 * q4, 0)))
                tb_ = b.I("pe", nc.tensor.matmul(ps_b[pb][:, 1, :], S["BTi"][rows, tile, :], ut[us][rows, tok], start=True, stop=True,
                                                 tile_position=(32 * q4, 0)))
                b.W("dve", tb_, ttab, w_free[pb])
                br, bi = ps_b[pb][:, 0, :], ps_b[pb][:, 1, :]
                b.I("dve", V.tensor_tensor(w[0][:], Cn[ts_][:], br, ALU.mult))
                b.I("dve", V.tensor_tensor(w[1][:], Sn[ts_][:], bi, ALU.mult))
                b.I("dve", V.tensor_tensor(w[0][:], w[0][:], w[1][:], ALU.add))
                b.I("dve", V.tensor_tensor(w[2][:], Cn[ts_][:], bi, ALU.mult))
                tpsb = b.I("dve", V.tensor_tensor(w[3][:], Sn[ts_][:], br, ALU.mult))
                psb_free[pb] = tpsb
                b.I("dve", V.tensor_tensor(w[2][:], w[2][:], w[3][:], ALU.subtract))
                if ps_ == 0:
                    i_re = init[:, q, 0:1] if init is not None else zero[:, 0:1]
                    i_im = init[:, q, 1:2] if init is not None else zero[:, 1:2]
                else:
                    i_re, i_im = xe[:, 4:5], xe[:, 5:6]
                rb = S["r"][:, q:q + 1].to_broadcast([128, 512])
                b.I("dve", V.tensor_tensor_scan(wre[pb][:], rb, w[0][:], i_re, ALU.mult, ALU.add))
                tsc = b.I("dve", V.tensor_tensor_scan(wim[pb][:], rb, w[2][:], i_im, ALU.mult, ALU.add))
                b.I("dve", V.tensor_tensor(xe[:, 0:1], Cn[ts_][:, 511:512], wre[pb][:, 511:512], ALU.mult))
                b.I("dve", V.tensor_tensor(xe[:, 1:2], Sn[ts_][:, 511:512], wim[pb][:, 511:512], ALU.mult))
                b.I("dve", V.tensor_tensor(xe[:, 2:3], Sn[ts_][:, 511:512], wre[pb][:, 511:512], ALU.mult))
                b.I("dve", V.tensor_tensor(xe[:, 3:4], Cn[ts_][:, 511:512], wim[pb][:, 511:512], ALU.mult))
                if ps_ == 0:
                    b.I("dve", V.tensor_tensor(xe[:, 4:5], xe[:, 0:1], xe[:, 1:2], ALU.subtract))
                    tx = b.I("dve", V.tensor_tensor(xe[:, 5:6], xe[:, 2:3], xe[:, 3:4], ALU.add))
                    b.Wf("dve", tx)
                elif not phaseB:
                    b.I("dve", V.tensor_tensor(E_out[:, q, 0:1], xe[:, 0:1], xe[:, 1:2], ALU.subtract))
                    b.I("dve", V.tensor_tensor(E_out[:, q, 1:2], xe[:, 2:3], xe[:, 3:4], ALU.add))
                tdve = b.last("dve")
                if phaseB:
                    G = nc.gpsimd
                    b.W("pool", tsc, ttab, x_free[pb])
                    b.I("pool", G.tensor_tensor(pw[0][:], Cn[ts_][:], wre[pb][:], ALU.mult))
                    b.I("pool", G.tensor_tensor(pw[1][:], Sn[ts_][:], wim[pb][:], ALU.mult))
                    b.I("pool", G.tensor_tensor(xr_[pb][:], pw[0][:], pw[1][:], ALU.subtract))
                    b.I("pool", G.tensor_tensor(pw[2][:], Sn[ts_][:], wre[pb][:], ALU.mult))
                    b.I("pool", G.tensor_tensor(pw[3][:], Cn[ts_][:], wim[pb][:], ALU.mult))
                    tpo = b.I("pool", G.tensor_tensor(xi_[pb][:], pw[2][:], pw[3][:], ALU.add))
                    w_free[pb] = tpo
                    b.W("pe", tpo, S["c_ready"])
                    orow = slice(32 * q4, 32 * q4 + 32)
                    b.I("pe", nc.tensor.matmul(ps_y[ps_][orow, :], S["CTr"][:, q, :], xr_[pb][:], start=True, stop=False,
                                               tile_position=(0, 32 * q4)))
                    x_free[pb] = b.I("pe", nc.tensor.matmul(ps_y[ps_][orow, :], S["CTi"][:, q, :], xi_[pb][:], start=False, stop=True,
                                                            tile_position=(0, 32 * q4)))
                    tab_last = [tdve, tpo]
                else:
                    tab_last = [tdve]
            tab_free[ts_] = tab_last
        if phaseB:
            ty = b.last("pe")
            for ps_ in range(2):
                b.W("dve", ty, yf_free[ps_])
                t1 = b.I("dve", V.scalar_tensor_tensor(yf[ps_][:], ut[us][:, ps_ * 512:(ps_ + 1) * 512], dcol[:, tile:tile + 1],
                                                       ps_y[ps_][:], ALU.mult, ALU.add))
                psy_free[ps_] = t1
                b.W("act", t1, yb_free[ps_])
                t2 = b.I("act", nc.scalar.activation(yb[ps_][:], yf[ps_][:], AF.Gelu))
                yf_free[ps_] = t2
                b.W("sp", t2)
                yb_free[ps_] = b.dma("sp", f"yb{ps_}", YT[tile * 128:(tile + 1) * 128, ps_ * 512:(ps_ + 1) * 512], yb[ps_][:])
            ut_free[us] = [b.last("pe"), b.last("dve")]
        else:
            ut_free[us] = b.last("pe")
    b.barrier()
    b.pop()


def layer1_a(b, c, h1_src):
    nc = b.nc
    ng1 = b.din("ng1c", [128, 32])
    w_in = b.din("ssm_w_in", [D, 2 * E])
    UT = b.dram["UT_d"]
    GsT = b.dram["GsT_d"]
    Eo = b.dram["E_d"]
    b.push()
    big1 = b.sb("big1b", [128, 32 * T], BF16)
    hT = big1[:].rearrange("p (c t) -> p c t", t=T)
    gcol = b.sb("gcol1", [128, 32], F32)
    t = b.dma("sp", "misc", gcol[:], ng1)
    b.W("dve", t)
    norm_T(b, c, h1_src, T // 128, gcol, hT, "n1")
    b.push()
    stg = [b.sb(f"ustg{i}", [128, T], BF16) for i in range(2)]
    st_free = [None, None]
    state = {"q": 0}

    def pieces(n):
        return [(0, 512), (512, 512)]

    def epi(n, ix, ps, tpe):
        qs = state["q"] % 2
        state["q"] += 1
        b.W("act", tpe, st_free[qs])
        for j in range(2):
            if n < 64:
                t1 = b.I("act", nc.scalar.copy(stg[qs][:, j * 512:(j + 1) * 512], ps[:, j, :]))
            else:
                t1 = b.I("act", nc.scalar.activation(stg[qs][:, j * 512:(j + 1) * 512], ps[:, j, :], AF.Silu))
        b.W("sp", t1)
        dst = UT if n < 64 else GsT
        st_free[qs] = b.dma("sp", f"ustg{qs}", dst[(n % 64) * 128:(n % 64 + 1) * 128, :], stg[qs][:])
        return [t1]

    fm_matmul(b, w_in, 32, list(range(128)), hT, pieces, epi, "sin")
    b.barrier()
    b.pop()
    b.pop()
    b.push()
    S = ssm_setup(b, c, False)
    Es = b.sb("E_sb", [128, NPAIR, 2], F32)
    ssm_scan(b, c, S, UT, False, None, E_out=Es)
    t = b.dma("sp", "misc", Eo, Es[:])
    b.W("sp", t)
    b.barrier()
    b.pop()


def build_l01():
    b = Bld()
    b.dout("h1", [T, D])
    b.dscr("UT_d", [E, T], BF16, external=True)
    b.dscr("GsT_d", [E, T], BF16, external=True)
    b.dscr("E_d", [128, NPAIR, 2], F32, external=True)
    c = stage_consts(b)
    layer0(b, c)
    layer1_a(b, c, b.dram["h1"])
    b.barrier()
    return b


def ssm_inputs(ssm_log_dt, ssm_lam_re, ssm_lam_im, ssm_b_re, ssm_b_im):
    ld = ssm_log_dt[0].reshape(NPAIR, 2)
    ldt = np.repeat(ld.T, 64, axis=0)
    def sm(a):
        return np.ascontiguousarray(a.reshape(NPAIR, 2, 64).transpose(1, 2, 0).reshape(128, NPAIR), np.float32)
    def bt(bm):
        v = bm.reshape(64, 4, 2, 64, 16)
        o = np.zeros((4, 2, 16, 64, 2, 64), np.float32)
        for par in range(2):
            o[:, par, :, :, par, :] = v[:, :, par, :, :].transpose(1, 3, 0, 2)
        return o.reshape(128, 64, 128)
    return {"ssm_ldt": np.ascontiguousarray(ldt, np.float32), "ssm_lre": sm(ssm_lam_re[0]), "ssm_lim": sm(ssm_lam_im[0]),
            "ssm_bt_re": bt(ssm_b_re[0]), "ssm_bt_im": bt(ssm_b_im[0])}


def layer1_b(b, c):
    nc = b.nc
    V = nc.vector
    h1 = b.din("h1", [T, D])
    UT = b.din("UT_d", [E, T], BF16)
    GsT = b.din("GsT_d", [E, T], BF16)
    Eall = b.din("E_all", [3, 128, NPAIR, 2])
    msk = b.din("msk", [128, 4])
    dsk = b.din("ssm_dcol", [128, 64])
    w_glu = b.din("ssm_w_glu", [E, E])
    w_out = b.din("ssm_w_out", [E, D])
    out = b.dram["out"]
    YT = b.dscr("YT_d", [E, T], BF16)
    ZT = b.dscr("ZT_d", [E, T], BF16)

    b.push()
    S = ssm_setup(b, c, True)
    ssm_csetup(b, c, S)
    dcol = b.sb("dcol", [128, 64], F32)
    mk = b.sb("mk", [128, 4], F32)
    Ea = b.sb("Ea", [128, 3, NPAIR, 2], F32)
    b.dma("sp", "misc", dcol[:], dsk)
    b.dma("sp", "misc", mk[:], msk)
    for i in range(3):
        t = b.dma("sp", "misc", Ea[:, i, :, :], Eall[i])
    pr = b.sb("pw_r", [128, NPAIR], F32)
    pi_ = b.sb("pw_i", [128, NPAIR], F32)
    q0 = b.sb("pq0", [128, NPAIR], F32)
    q1 = b.sb("pq1", [128, NPAIR], F32)
    q2 = b.sb("pq2", [128, NPAIR], F32)
    Sk = b.sb("Sk", [128, NPAIR, 2], F32)
    b.W("dve", t, S["ready"])
    b.I("dve", V.tensor_copy(pr[:], S["ar"][:]))
    b.I("dve", V.tensor_copy(pi_[:], S["ai"][:]))
    for _ in range(10):
        b.I("dve", V.tensor_tensor(q0[:], pr[:], pr[:], ALU.mult))
        b.I("dve", V.tensor_tensor(q1[:], pi_[:], pi_[:], ALU.mult))
        b.I("dve", V.tensor_tensor(q2[:], pr[:], pi_[:], ALU.mult))
        b.I("dve", V.tensor_tensor(pr[:], q0[:], q1[:], ALU.subtract))
        b.I("dve", V.tensor_scalar(pi_[:], q2[:], 2.0, None, ALU.mult))
    b.I("dve", V.memset(Sk[:], 0.0))
    sr, si = Sk[:, :, 0], Sk[:, :, 1]
    for i in range(3):
        b.I("dve", V.tensor_tensor(q0[:], pr[:], sr, ALU.mult))
        b.I("dve", V.tensor_tensor(q1[:], pi_[:], si, ALU.mult))
        b.I("dve", V.tensor_tensor(q0[:], q0[:], q1[:], ALU.subtract))
        b.I("dve", V.tensor_tensor(q0[:], q0[:], Ea[:, i, :, 0], ALU.add))
        b.I("dve", V.tensor_tensor(q1[:], pr[:], si, ALU.mult))
        b.I("dve", V.tensor_tensor(q2[:], pi_[:], sr, ALU.mult))
        b.I("dve", V.tensor_tensor(q1[:], q1[:], q2[:], ALU.add))
        b.I("dve", V.tensor_tensor(q1[:], q1[:], Ea[:, i, :, 1], ALU.add))
        b.I("dve", V.tensor_tensor(q0[:], q0[:], sr, ALU.subtract))
        b.I("dve", V.tensor_tensor(q1[:], q1[:], si, ALU.subtract))
        b.I("dve", V.scalar_tensor_tensor(sr, q0[:], mk[:, i:i + 1], sr, ALU.mult, ALU.add))
        tS = b.I("dve", V.scalar_tensor_tensor(si, q1[:], mk[:, i:i + 1], si, ALU.mult, ALU.add))
    b.Wf("dve", tS)
    ssm_scan(b, c, S, UT, True, Sk, YT=YT, dcol=dcol)
    b.pop()

    for ps_ in range(2):
        tok = slice(ps_ * 512, (ps_ + 1) * 512)
        b.push()
        yT = b.sb(f"yTbig{ps_}", [128, 64, 512], BF16)
        tl = b.dma("sp", "misc", yT[:], YT[:, tok].rearrange("(c p) t -> p c t", p=128))
        b.W("pe", tl)
        gs = [b.sb(f"gs{ps_}_{i}", [128, 512], BF16) for i in range(2)]
        sg = [b.sb(f"sg{ps_}_{i}", [128, 512], F32) for i in range(2)]
        zb = [b.sb(f"zb{ps_}_{i}", [128, 512], BF16) for i in range(2)]
        gs_free = [None, None]
        sg_free = [None, None]
        zb_free = [None, None]
        st = {"i": 0}

        def pieces(n):
            return [(0, 512)]

        def epi(n, ix, ps, tpe):
            i = st["i"] % 2
            st["i"] += 1
            b.W("sp", gs_free[i])
            tg = b.dma("sp", f"gs{i}", gs[i][:], GsT[n * 128:(n + 1) * 128, tok])
            b.W("act", tpe, sg_free[i])
            t1 = b.I("act", nc.scalar.activation(sg[i][:], ps[:, 0, :], AF.Sigmoid))
            b.W("dve", t1, tg, zb_free[i])
            b.I("dve", V.tensor_tensor(sg[i][:], sg[i][:], yT[:, n, :], ALU.mult))
            t2 = b.I("dve", V.tensor_tensor(zb[i][:], sg[i][:], gs[i][:], ALU.mult))
            sg_free[i] = t2
            gs_free[i] = t2
            b.W("sp", t2)
            zb_free[i] = b.dma("sp", f"zb{i}", ZT[n * 128:(n + 1) * 128, tok], zb[i][:])
            return [t1]

        fm_matmul(b, w_glu, 64, list(range(64)), yT, pieces, epi, f"glu{ps_}", nslot=2)
        b.barrier()
        b.pop()

    for ps_ in range(2):
        b.push()
        zT = b.sb(f"zTbig{ps_}", [128, 64, 512], BF16)
        tl = b.dma("sp", "misc", zT[:], ZT[:, ps_ * 512:(ps_ + 1) * 512].rearrange("(c p) t -> p c t", p=128))
        b.W("pe", tl)
        xr = [b.sb(f"hres{ps_}_{i}", [128, 256], F32) for i in range(2)]
        hr = [b.sb(f"hout{ps_}_{i}", [128, 256], F32) for i in range(2)]
        xr_free = [None, None]
        hr_free = [None, None]
        st = {"i": 0}

        def epi_out(s_, t, ps, tpe):
            i = st["i"] % 2
            st["i"] += 1
            r0 = ps_ * 512 + t * 128
            b.W("sp", xr_free[i])
            tl_ = b.dma("sp", f"hres{i}", xr[i][:], h1[r0:r0 + 128, s_ * 256:(s_ + 1) * 256])
            b.W("dve", tpe, tl_, hr_free[i])
            ta = b.I("dve", V.tensor_tensor(hr[i][:], ps[:, 0:256], xr[i][:], ALU.add))
            xr_free[i] = ta
            b.W("sp", ta)
            hr_free[i] = b.dma("sp", f"hout{i}", out[r0:r0 + 128, s_ * 256:(s_ + 1) * 256], hr[i][:])
            return [ta]

        tm_matmul(b, w_out, 64, 16, 256, zT, 4, epi_out, f"sout{ps_}")
        b.barrier()
        b.pop()


def build_l1b():
    b = Bld()
    b.dout("out", [T, D])
    c = stage_consts(b)
    layer1_b(b, c)
    b.barrier()
    return b


def csm(cm):
    v = cm.reshape(NPAIR, 2, 16, 64)
    o = np.zeros((2, 64, NPAIR, 2, 16), np.float32)
    for par in range(2):
        o[par, :, :, par, :] = v[:, par, :, :].transpose(2, 0, 1)
    return o.reshape(128, NPAIR, 32)


_cache = {}


def kernel(x, norm_g, attn_w_in, attn_q_norm_g, attn_k_norm_g, attn_sinks, attn_w_out,
           ssm_w_in, ssm_log_dt, ssm_lam_re, ssm_lam_im, ssm_b_re, ssm_b_im,
           ssm_c_re, ssm_c_im, ssm_d, ssm_w_glu, ssm_w_out):
    f = lambda a: np.asarray(a, np.float32)
    x, norm_g = f(x), f(norm_g)
    if "l01" not in _cache:
        _cache["l01"] = build_l01()
        _cache["l1b"] = build_l1b()
    ssm_in = ssm_inputs(f(ssm_log_dt), f(ssm_lam_re), f(ssm_lam_im), f(ssm_b_re), f(ssm_b_im))
    ng1c = np.ascontiguousarray(norm_g[1].reshape(32, 128).T, np.float32)
    maps = []
    for core in range(NCORES):
        m = l0_inputs(core, x, norm_g, f(attn_w_in), f(attn_q_norm_g), f(attn_k_norm_g), f(attn_sinks), f(attn_w_out))
        m.update(ssm_in)
        m["ng1c"] = ng1c
        m["ssm_w_in"] = f(ssm_w_in)[0]
        maps.append(m)
    r1 = run_bass_kernel_spmd(_cache["l01"].nc, maps, core_ids=list(range(NCORES))).results
    _cache["r1"] = r1
    del maps
    cs_re, cs_im = csm(f(ssm_c_re)[0]), csm(f(ssm_c_im)[0])
    dcol = np.ascontiguousarray(f(ssm_d)[0].reshape(64, 128).T, np.float32)
    maps = []
    for core in range(NCORES):
        bi, ch = divmod(core, 4)
        Eall = np.zeros((3, 128, NPAIR, 2), np.float32)
        msk = np.zeros((128, 4), np.float32)
        for i in range(3):
            Eall[i] = r1[bi * 4 + i]["E_d"]
            msk[:, i] = 1.0 if i < ch else 0.0
        m = dict(ssm_in)
        m.update({"h1": r1[core]["h1"], "UT_d": r1[core]["UT_d"], "GsT_d": r1[core]["GsT_d"], "E_all": Eall, "msk": msk,
                  "ssm_dcol": dcol, "ssm_cs_re": cs_re, "ssm_cs_im": cs_im,
                  "ssm_w_glu": f(ssm_w_glu)[0], "ssm_w_out": f(ssm_w_out)[0]})
        maps.append(m)
    r2 = run_bass_kernel_spmd(_cache["l1b"].nc, maps, core_ids=list(range(NCORES))).results
    out = np.empty((2, 4096, D), np.float32)
    for core in range(NCORES):
        bi, ch = divmod(core, 4)
        out[bi, ch * T:(ch + 1) * T] = r2[core]["out"]
    return out
```
